# Optimizing a Trainium2 kernel written in Bass

```python
import jax, jax.numpy as jnp
from jax import lax
import numpy as np

D_MODEL = 1024
BATCH = 16
SEQ = 4096
DEPTH = 2

N_MIXERS = 2
N_MEM = 256
MIX_WIDTH = D_MODEL
MEM_HEADS = 4
MEM_WIDTH = MIX_WIDTH // 4
MEM_HEAD_DIM = MEM_WIDTH // MEM_HEADS
MAIN_WIDTH = MIX_WIDTH - MEM_WIDTH
MLA_HEADS = 12
QK_NOPE = 64
QK_ROPE = 32
V_HEAD = MAIN_WIDTH // MLA_HEADS
Q_LORA = (3 * D_MODEL) // 8
KV_LORA = D_MODEL // 4
ROPE_THETA = 10000.0
Q_BLOCK = 128
FNET_GROUPS = 4
FNET_GROUP_DIM = MAIN_WIDTH // FNET_GROUPS
EPS = 1e-6

MLA_IN = Q_LORA + KV_LORA + QK_ROPE + MEM_WIDTH + MIX_WIDTH
FNET_IN = MAIN_WIDTH + MEM_WIDTH + MIX_WIDTH

kernel_name = "hybrid_mla_fnet_memory_encoder"


def rmsnorm(x, g):
    xf = x.astype(jnp.float32)
    y = xf * lax.rsqrt(jnp.mean(xf * xf, axis=-1, keepdims=True) + EPS)
    return (y * g.astype(jnp.float32)).astype(x.dtype)


def rope_tables(positions):
    inv_freq = 1.0 / (ROPE_THETA ** (jnp.arange(0, QK_ROPE, 2, dtype=jnp.float32) / QK_ROPE))
    ang = positions.astype(jnp.float32)[..., None] * inv_freq
    return jnp.cos(ang), jnp.sin(ang)


def apply_rope(t, cos, sin):
    tf = t.astype(jnp.float32)
    t1, t2 = tf[..., : QK_ROPE // 2], tf[..., QK_ROPE // 2:]
    return jnp.concatenate([t1 * cos - t2 * sin, t2 * cos + t1 * sin], axis=-1).astype(t.dtype)


def mla_attention(q_nope, q_rope, k_nope, k_rope, v):
    B, S, H, _ = q_nope.shape
    nb = S // Q_BLOCK
    scale = 1.0 / float(np.sqrt(QK_NOPE + QK_ROPE))

    def to_blocks(t):
        return jnp.moveaxis(t.reshape((B, nb, Q_BLOCK) + t.shape[2:]), 1, 0)

    def one_block(args):
        qn, qr = args
        s = (jnp.einsum('bqhd,bkhd->bhqk', qn, k_nope)
             + jnp.einsum('bqhr,bkr->bhqk', qr, k_rope)).astype(jnp.float32) * scale
        p = jax.nn.softmax(s, axis=-1).astype(v.dtype)
        return jnp.einsum('bhqk,bkhd->bqhd', p, v)

    out = lax.map(one_block, (to_blocks(q_nope), to_blocks(q_rope)))
    return jnp.moveaxis(out, 0, 1).reshape(B, S, H * V_HEAD)


def memory_cross_attention(q_mem, mem, mem_norm_g, w_mem_kv):
    B, S, _ = q_mem.shape
    q = q_mem.reshape(B, S, MEM_HEADS, MEM_HEAD_DIM)
    kv = (rmsnorm(mem, mem_norm_g) @ w_mem_kv).reshape(B, mem.shape[1], 2, MEM_HEADS, MEM_HEAD_DIM)
    k, v = kv[:, :, 0], kv[:, :, 1]
    s = jnp.einsum('bshd,bmhd->bhsm', q, k).astype(jnp.float32) * (1.0 / float(np.sqrt(MEM_HEAD_DIM)))
    p = jax.nn.softmax(s, axis=-1).astype(v.dtype)
    return jnp.einsum('bhsm,bmhd->bshd', p, v).reshape(B, S, MEM_WIDTH)


def mla_layer(x, mem, cos, sin, norm_g, w_in, q_norm_g, kv_norm_g, w_uq, w_ukv,
              mem_norm_g, w_mem_kv, w_out):
    B, S, _ = x.shape
    h = rmsnorm(x, norm_g)
    proj = h @ w_in
    o1 = Q_LORA
    o2 = o1 + KV_LORA
    o3 = o2 + QK_ROPE
    o4 = o3 + MEM_WIDTH
    c_q, c_kv, k_rope = proj[..., :o1], proj[..., o1:o2], proj[..., o2:o3]
    q_mem, gate = proj[..., o3:o4], proj[..., o4:]
    q = (rmsnorm(c_q, q_norm_g) @ w_uq).reshape(B, S, MLA_HEADS, QK_NOPE + QK_ROPE)
    kv = (rmsnorm(c_kv, kv_norm_g) @ w_ukv).reshape(B, S, MLA_HEADS, QK_NOPE + V_HEAD)
    q_nope = q[..., :QK_NOPE]
    q_rope = apply_rope(q[..., QK_NOPE:], cos[:, :, None, :], sin[:, :, None, :])
    k_nope, v = kv[..., :QK_NOPE], kv[..., QK_NOPE:]
    k_rope = apply_rope(k_rope, cos, sin)
    attn = mla_attention(q_nope, q_rope, k_nope, k_rope, v)
    mem_out = memory_cross_attention(q_mem, mem, mem_norm_g, w_mem_kv)
    branch = jnp.concatenate([attn, mem_out], axis=-1) * jax.nn.silu(gate)
    return x + branch @ w_out


def fnet_layer(x, mem, norm_g, w_in, w_fnet, mem_norm_g, w_mem_kv, w_out):
    B, S, _ = x.shape
    h = rmsnorm(x, norm_g)
    proj = h @ w_in
    f = proj[..., :MAIN_WIDTH].reshape(B, S, FNET_GROUPS, FNET_GROUP_DIM).astype(jnp.float32)
    q_mem = proj[..., MAIN_WIDTH:MAIN_WIDTH + MEM_WIDTH]
    gate = proj[..., MAIN_WIDTH + MEM_WIDTH:]
    spec = jnp.fft.fft2(f, axes=(1, 3), norm='ortho').real
    mixed = jnp.einsum('bsgc,gcd->bsgd', spec, w_fnet.astype(jnp.float32))
    mixed = mixed.reshape(B, S, MAIN_WIDTH).astype(x.dtype)
    mem_out = memory_cross_attention(q_mem, mem, mem_norm_g, w_mem_kv)
    branch = jnp.concatenate([mixed, mem_out], axis=-1) * jax.nn.silu(gate)
    return x + branch @ w_out


def setup_inputs(seed: int = 0) -> dict:
    key = jax.random.key(seed)
    ks = jax.random.split(key, 24)
    f32 = jnp.float32

    def w(k, shape, fan_in):
        return jax.random.normal(k, shape, f32) * (fan_in ** -0.5)

    def gain(k, n):
        return 1.0 + 0.02 * jax.random.normal(k, (n,), f32)

    x = jax.random.normal(ks[0], (BATCH, SEQ, D_MODEL), f32)
    mem = jax.random.normal(ks[1], (BATCH, N_MEM, D_MODEL), f32)
    offsets = jax.random.randint(ks[2], (BATCH, 1), 0, 1024, dtype=jnp.int32)
    positions = (jnp.arange(SEQ, dtype=jnp.int32)[None, :] + offsets).astype(jnp.int32)
    return {
        "x": x,
        "mem": mem,
        "positions": positions,
        "norm_g_l0": gain(ks[3], D_MODEL),
        "w_in_l0": w(ks[4], (D_MODEL, MLA_IN), D_MODEL),
        "q_norm_g_l0": gain(ks[5], Q_LORA),
        "kv_norm_g_l0": gain(ks[6], KV_LORA),
        "w_uq_l0": w(ks[7], (Q_LORA, MLA_HEADS * (QK_NOPE + QK_ROPE)), Q_LORA),
        "w_ukv_l0": w(ks[8], (KV_LORA, MLA_HEADS * (QK_NOPE + V_HEAD)), KV_LORA),
        "mem_norm_g_l0": gain(ks[9], D_MODEL),
        "w_mem_kv_l0": w(ks[10], (D_MODEL, 2 * MEM_WIDTH), D_MODEL),
        "w_out_l0": w(ks[11], (MIX_WIDTH, D_MODEL), MIX_WIDTH),
        "norm_g_l1": gain(ks[12], D_MODEL),
        "w_in_l1": w(ks[13], (D_MODEL, FNET_IN), D_MODEL),
        "w_fnet_l1": w(ks[14], (FNET_GROUPS, FNET_GROUP_DIM, FNET_GROUP_DIM), FNET_GROUP_DIM),
        "mem_norm_g_l1": gain(ks[15], D_MODEL),
        "w_mem_kv_l1": w(ks[16], (D_MODEL, 2 * MEM_WIDTH), D_MODEL),
        "w_out_l1": w(ks[17], (MIX_WIDTH, D_MODEL), MIX_WIDTH),
        "final_norm_g": gain(ks[18], D_MODEL),
    }


def reference(x, mem, positions,
              norm_g_l0, w_in_l0, q_norm_g_l0, kv_norm_g_l0, w_uq_l0, w_ukv_l0,
              mem_norm_g_l0, w_mem_kv_l0, w_out_l0,
              norm_g_l1, w_in_l1, w_fnet_l1, mem_norm_g_l1, w_mem_kv_l1, w_out_l1,
              final_norm_g):
    cos, sin = rope_tables(positions)
    for i in range(DEPTH):
        if i % N_MIXERS == 0:
            x = mla_layer(x, mem, cos, sin, norm_g_l0, w_in_l0, q_norm_g_l0, kv_norm_g_l0,
                          w_uq_l0, w_ukv_l0, mem_norm_g_l0, w_mem_kv_l0, w_out_l0)
        else:
            x = fnet_layer(x, mem, norm_g_l1, w_in_l1, w_fnet_l1,
                           mem_norm_g_l1, w_mem_kv_l1, w_out_l1)
    return rmsnorm(x, final_norm_g)
```

```python
import contextlib
import os
import numpy as np
import ml_dtypes
import concourse.bass as bass
import concourse.mybir as mybir
from concourse.bass_utils import run_bass_kernel_spmd

F32 = mybir.dt.float32
BF16 = mybir.dt.bfloat16
I32 = mybir.dt.int32
AF = mybir.ActivationFunctionType
ALU = mybir.AluOpType
BF = ml_dtypes.bfloat16

S = 4096
D = 1024
NT = int(os.environ.get("KNT", S // 128))
KHEADS = int(os.environ.get("KHEADS", 12))
KQC = int(os.environ.get("KQC", 8))
EPS = 1e-6
PI = float(np.pi)
VW = 80
ENGS = ("pe", "act", "dve", "pool", "sp")


class Sem:
    def __init__(self, h):
        self.h = h
        self.n = 0


class TB:
    def __init__(self, ap, excl=False):
        self.ap = ap
        self.w = None
        self.r = []
        self.excl = excl

    def __getitem__(self, k):
        return self.ap[k]


class Pool:
    def __init__(self, banks):
        self.banks = banks
        self.i = 0

    def next(self):
        b = self.banks[self.i % len(self.banks)]
        self.i += 1
        return b


class Prog:
    def __init__(self, nc, stack):
        self.nc = nc
        self.stack = stack
        self.q = {e: [] for e in ENGS}
        self.seen = {e: {} for e in ENGS}
        self.sems = {}
        self.nops = 0
        self._grp = None
        self.limit = int(os.environ.get("KLIMIT", 10 ** 9))
        self.esem = {e: self.sem("e_" + e) for e in ("pe", "act", "dve", "pool")}
        self.arena = stack.enter_context(nc.sbuf_tensor("arena", [128, 102400], BF16))
        self.off = 0
        self.ps = stack.enter_context(nc.psum_tensor("ps", [128, 4096], F32))
        self.banks = [TB(self.ps[:, i * 512:(i + 1) * 512], excl=True) for i in range(8)]

    def sem(self, name):
        if name not in self.sems:
            self.sems[name] = Sem(self.stack.enter_context(self.nc.semaphore(name)))
        return self.sems[name]

    def alloc(self, free_shape, dt):
        n = int(np.prod(free_shape))
        nbytes = n * (2 if dt == BF16 else 4)
        nbytes = (nbytes + 63) // 64 * 64
        assert self.off + nbytes <= 102400 * 2, ("SBUF arena overflow", self.off, nbytes)
        a = self.arena[:, self.off // 2:(self.off + nbytes) // 2]
        self.off += nbytes
        if dt != BF16:
            a = a.bitcast(dt)
        a = a[:, 0:n]
        if len(free_shape) == 2:
            a = a.rearrange("p (a b) -> p a b", a=free_shape[0])
        elif len(free_shape) == 3:
            a = a.rearrange("p (a b c) -> p a b c", a=free_shape[0], b=free_shape[1])
        return TB(a)

    def _waits(self, eng, deps):
        mx = {}
        for d in deps:
            if d is None:
                continue
            mx[d[0]] = max(mx.get(d[0], 0), d[1])
        for s, v in mx.items():
            if self.seen[eng].get(s, 0) < v:
                self.q[eng].append(("w", s, v))
                self.seen[eng][s] = v

    def _deps(self, r, w):
        deps = []
        w = list(w) + [b for b in r if b.excl]
        for b in r:
            deps.append(b.w)
        for b in w:
            deps.append(b.w)
            deps.extend(b.r)
        return deps

    def _commit(self, tok, r, w):
        w = list(w) + [b for b in r if b.excl]
        r = [b for b in r if not b.excl]
        for b in r:
            b.r.append(tok)
        for b in w:
            b.w = tok
            b.r = []

    def op(self, eng, fn, r=(), w=()):
        self.nops += 1
        if self.nops > self.limit:
            return None
        self._waits(eng, self._deps(r, w))
        s = self.esem[eng]
        s.n += 1
        self.q[eng].append(("o", fn, s))
        tok = (s, s.n)
        self._commit(tok, r, w)
        return tok

    def dma(self, eng, out, in_, sem, r=(), w=()):
        self.nops += 1
        if self.nops > self.limit:
            return None
        self._waits(eng, self._deps(r, w))
        sem.n += 16
        self.q[eng].append(("d", out, in_, sem))
        tok = (sem, sem.n)
        self._commit(tok, r, w)
        if self._grp is not None:
            self._grp.extend((b, sem) for b in w)
        return tok

    def group_begin(self):
        self._grp = []

    def group_end(self):
        for b, sem in self._grp:
            b.w = (sem, sem.n)
        self._grp = None

    def barrier(self):
        toks = [(s, s.n) for s in self.sems.values() if s.n > 0]
        for e in ENGS:
            self._waits(e, toks)

    def act(self, out, in_, func, r, w, scale=1.0, accum=None):
        def fn(e):
            if accum is None:
                return e.activation(out=out, in_=in_, func=func, scale=scale)
            return e.activation(out=out, in_=in_, func=func, scale=scale, accum_out=accum)
        return self.op("act", fn, r, w)

    def copy(self, eng, out, in_, r, w):
        if eng == "act":
            return self.op("act", lambda e: e.activation(out=out, in_=in_, func=AF.Copy), r, w)
        return self.op(eng, lambda e: e.tensor_copy(out=out, in_=in_), r, w)

    def tt(self, eng, out, in0, in1, op, r, w):
        return self.op(eng, lambda e: e.tensor_tensor(out=out, in0=in0, in1=in1, op=op), r, w)

    def ts(self, eng, out, in0, s1, s2, op0, op1, r, w):
        if s2 is None:
            return self.op(eng, lambda e: e.tensor_scalar(out=out, in0=in0, scalar1=s1, scalar2=None, op0=op0), r, w)
        return self.op(eng, lambda e: e.tensor_scalar(out=out, in0=in0, scalar1=s1, scalar2=s2, op0=op0, op1=op1), r, w)

    def stt(self, eng, out, in0, scalar, in1, op0, op1, r, w):
        return self.op(eng, lambda e: e.scalar_tensor_tensor(out=out, in0=in0, scalar=scalar, in1=in1, op0=op0, op1=op1), r, w)

    def mm(self, items, r, w):
        items = list(items)

        def fn(e):
            ins = None
            for (o, l, rh, st, sp) in items:
                ins = e.matmul(o, lhsT=l, rhs=rh, start=st, stop=sp)
            return ins
        return self.op("pe", fn, r, w)

    def tr(self, items, ident, r, w):
        items = list(items)

        def fn(e):
            ins = None
            for (o, i) in items:
                ins = e.transpose(out=o, in_=i, identity=ident[0:i.shape[0], 0:i.shape[0]])
            return ins
        return self.op("pe", fn, r, w)

    def emit(self):
        nc = self.nc
        engmap = {"pe": "tensor", "act": "scalar", "dve": "vector", "pool": "gpsimd", "sp": "sync"}
        with nc.Block() as block:
            for e in ENGS:
                items = self.q[e]

                def body(eng, items=items):
                    for it in items:
                        if it[0] == "w":
                            eng.wait_ge(it[1].h, it[2])
                        elif it[0] == "o":
                            it[1](eng).then_inc(it[2].h, 1)
                        else:
                            eng.dma_start(out=it[1], in_=it[2]).then_inc(it[3].h, 16)
                getattr(block, engmap[e])(body)


def pipeline(stages, n):
    ns = len(stages)
    for step in range(n + ns - 1):
        for k, st in enumerate(stages):
            i = step - k
            if 0 <= i < n:
                st(i)


def build_program(NB=2, dbg=False, stop_after=None):
    nc = bass.Bass("TRN2", target_bir_lowering=False)

    def din(name, shape, dt):
        return nc.dram_tensor(name, shape, dt, kind="ExternalInput").ap()

    def dscr(name, shape, dt):
        ext = bool(dbg) and (dbg is True or name in dbg)
        return nc.dram_tensor(name, shape, dt, kind="ExternalOutput" if ext else "Internal").ap()

    x = din("x", [2, S, D], F32)
    mem = din("mem", [2, 256, D], F32)
    pos = din("pos", [2, 128, 32], I32)
    w_in0 = din("w_in0", [128, 8, 1952], F32)
    w_uq = din("w_uq", [128, 3, 1152], F32)
    w_ukv = din("w_ukv", [128, 2, 1536], F32)
    w_out0 = din("w_out0", [128, 8, 1024], F32)
    w_out1 = din("w_out1", [128, 8, 1024], F32)
    w_mem = [din("w_mem0", [128, 8, 512], F32), din("w_mem1", [128, 8, 512], F32)]
    w_in1 = din("w_in1", [128, 8, 2048], F32)
    w_fn = din("w_fn", [192, 768], F32)
    g0 = din("g0", [128, 1024], F32)
    g1 = din("g1", [128, 1024], F32)
    gf = din("gf", [128, 1024], F32)
    gm = [din("gm0", [128, 1024], F32), din("gm1", [128, 1024], F32)]
    gq = din("gq", [128, 384], F32)
    gkv = din("gkv", [128, 256], F32)
    c_idb = din("c_idb", [128, 128], BF16)
    c_idf = din("c_idf", [128, 128], F32)
    c_invf = din("c_invf", [128, 16], F32)
    c_cc = din("c_cc", [192, 192], F32)
    c_sc = din("c_sc", [192, 192], F32)
    c_t1 = din("c_t1", [128, 64, 128], BF16)
    c_d2 = din("c_d2", [128, 64], BF16)
    y = nc.dram_tensor("y", [2, S, D], F32, kind="ExternalOutput").ap()

    QT = dscr("s_qt", [2, 12, 96, S], BF16)
    KN = dscr("s_kn", [2, 6, 128, S], BF16)
    KR = dscr("s_kr", [2, 32, S], BF16)
    VV = dscr("s_vv", [2, S, 12 * VW], BF16)
    QM = [dscr("s_qm0", [2, 2, 128, S], BF16), dscr("s_qm1", [2, 2, 128, S], BF16)]
    SG = [dscr("s_sg0", [2, S, D], BF16), dscr("s_sg1", [2, S, D], BF16)]
    ATT = [dscr("s_att0", [2, S, D], BF16), dscr("s_att1", [2, S, D], BF16)]
    X1 = dscr("s_x1", [2, S, D], F32)
    ZZ = dscr("s_zz", [2, S, 2, 768], BF16)
    BS = dscr("s_bs", [2, 2, 64, 64, 768], BF16)

    stack = contextlib.ExitStack()
    with stack:
        p = Prog(nc, stack)
        LD = lambda n: p.sem("ld_" + n)
        ST = lambda n: p.sem("st_" + n)

        idb = p.alloc([128], BF16)
        idf = p.alloc([128], F32)
        nhalf = p.alloc([4], F32)
        Km = [[p.alloc([4, 256], BF16) for b in range(2)] for l in range(2)]
        Vm = [[p.alloc([2, 4, VW], BF16) for b in range(2)] for l in range(2)]
        junks = Pool([p.alloc([1024], BF16) for i in range(4)])
        base_off = p.off
        p.group_begin()
        p.dma("sp", idb.ap, c_idb[:, :], LD("c"), w=[idb])
        p.dma("sp", idf.ap, c_idf[:, :], LD("c"), w=[idf])
        p.group_end()
        p.op("pool", lambda e: e.memset(nhalf.ap, -0.5), w=[nhalf])
        for l in range(2):
            for b in range(2):
                p.op("pool", lambda e, t=Vm[l][b]: e.memset(t.ap, 1.0), w=[Vm[l][b]])

        def load_w(dst, src, nk, name):
            for kc in range(nk):
                p.dma("pool", dst[:, kc, :], src[:, kc, :], LD(name), w=[dst])

        def rstd_chain(st, r):
            p.ts("dve", st[:, 1:2], st[:, 0:1], EPS, None, ALU.add, None, r=[st] + r, w=[st])
            p.tt("pool", st[:, 2:3], st[:, 1:2], nhalf[:, 0:1], ALU.pow, r=[st, nhalf], w=[st])

        def phase_M():
            p.off = base_off
            T = Pool(p.banks[0:2])
            Fp = Pool(p.banks[2:8])
            gmb = [p.alloc([1024], F32) for l in range(2)]
            wmb = [p.alloc([8, 512], BF16) for l in range(2)]
            p.group_begin()
            for l in range(2):
                p.dma("sp", gmb[l].ap, gm[l][:, :], LD("g"), w=[gmb[l]])
                load_w(wmb[l], w_mem[l], 8, "w")
            p.group_end()
            mt = [p.alloc([1024], F32) for i in range(2)]
            st = [p.alloc([8], F32) for i in range(2)]
            hb = [p.alloc([1024], BF16) for i in range(2)]
            hT = [p.alloc([8, 128], BF16) for i in range(2)]
            kb = [p.alloc([256], BF16) for i in range(2)]
            it = 0
            for b in range(NB):
                for mi in range(2):
                    s = it % 2
                    p.dma("sp", mt[s].ap, mem[b, mi * 128:(mi + 1) * 128, :], LD("mt%d" % s), w=[mt[s]])
                    jk = junks.next()
                    p.act(jk.ap, mt[s].ap, AF.Square, r=[mt[s]], w=[st[s], jk], scale=1.0 / 32, accum=st[s][:, 0:1])
                    rstd_chain(st[s], [])
                    for l in range(2):
                        s2 = it % 2
                        it += 1
                        p.stt("dve", hb[s2].ap, mt[s].ap, st[s][:, 2:3], gmb[l].ap, ALU.mult, ALU.mult,
                              r=[mt[s], st[s], gmb[l]], w=[hb[s2]])
                        tb = T.next()
                        tv = tb.ap.bitcast(BF16)
                        p.tr([(tv[:, k * 128:(k + 1) * 128], hb[s2][:, k * 128:(k + 1) * 128]) for k in range(8)],
                             idb.ap, r=[hb[s2], idb], w=[tb])
                        p.copy("act", hT[s2].ap.rearrange("p a b -> p (a b)"), tv, r=[tb], w=[hT[s2]])
                        fb = Fp.next()
                        p.mm([(fb.ap, hT[s2][:, k, :], wmb[l][:, k, :], k == 0, k == 7) for k in range(8)],
                             r=[hT[s2], wmb[l]], w=[fb])
                        p.copy("dve", kb[s2].ap, fb[:, 0:256], r=[fb], w=[kb[s2]])
                        p.copy("dve", Vm[l][b][:, mi, :, 0:64], fb[:, 256:512].rearrange("p (h d) -> p h d", h=4),
                               r=[fb], w=[Vm[l][b]])
                        tb = T.next()
                        tv = tb.ap.bitcast(BF16)
                        p.tr([(tv[:, j * 128:(j + 1) * 128], kb[s2][:, j * 128:(j + 1) * 128]) for j in range(2)],
                             idb.ap, r=[kb[s2], idb], w=[tb])
                        src = tv[:, 0:256].rearrange("p (j m) -> p j m", j=2)
                        p.copy("dve", Km[l][b][0:64, 0::2, mi * 128:(mi + 1) * 128], src[0:64], r=[tb], w=[Km[l][b]])
                        p.copy("dve", Km[l][b][0:64, 1::2, mi * 128:(mi + 1) * 128], src[64:128], r=[tb], w=[Km[l][b]])
            p.barrier()

        def phase_A0():
            p.off = base_off
            T = Pool(p.banks[0:2])
            Fp = Pool(p.banks[2:8])
            wi = p.alloc([8, 1952], BF16)
            wq = p.alloc([3, 1152], BF16)
            wkv = p.alloc([2, 1536], BF16)
            p.group_begin()
            load_w(wi, w_in0, 8, "w")
            load_w(wq, w_uq, 3, "w")
            load_w(wkv, w_ukv, 2, "w")
            g0b = p.alloc([1024], F32)
            gqb = p.alloc([384], F32)
            gkvb = p.alloc([256], F32)
            invf = p.alloc([16], F32)
            p.dma("sp", g0b.ap, g0[:, :], LD("g"), w=[g0b])
            p.dma("sp", gqb.ap, gq[:, :], LD("g"), w=[gqb])
            p.dma("sp", gkvb.ap, gkv[:, :], LD("g"), w=[gkvb])
            p.dma("sp", invf.ap, c_invf[:, :], LD("g"), w=[invf])
            p.group_end()
            cs = p.alloc([32, 64], F32)
            posi = p.alloc([32], I32)
            posf = p.alloc([32], F32)
            ang = p.alloc([32, 16], F32)
            rr = p.alloc([32, 16], F32)
            ki = p.alloc([32, 16], I32)
            kf = p.alloc([32, 16], F32)
            xt = [p.alloc([1024], F32) for i in range(2)]
            st1 = [p.alloc([8], F32) for i in range(2)]
            st2 = [p.alloc([8], F32) for i in range(2)]
            hb = [p.alloc([1024], BF16) for i in range(2)]
            hT = [p.alloc([8, 128], BF16) for i in range(2)]
            cn = [p.alloc([640], BF16) for i in range(2)]
            ext = [p.alloc([384], BF16) for i in range(2)]
            sg = [p.alloc([1024], BF16) for i in range(2)]
            tmpk = [p.alloc([2, 32], F32) for i in range(2)]
            tT = [p.alloc([8, 128], BF16) for i in range(2)]
            qb = [p.alloc([12, 96], BF16) for i in range(2)]
            rtmp = [p.alloc([2, 4, 32], F32) for i in range(2)]
            knb = [p.alloc([768], BF16) for i in range(2)]
            vb = [p.alloc([12, VW], BF16) for i in range(2)]
            qT = [p.alloc([12, 128], BF16) for i in range(2)]
            knT = [p.alloc([6, 128], BF16) for i in range(2)]
            for i in range(2):
                p.op("pool", lambda e, t=vb[i]: e.memset(t.ap, 1.0), w=[vb[i]])
                p.op("pool", lambda e, t=ext[i]: e.memset(t.ap, 0.0), w=[ext[i]])

            for b in range(NB):
                p.dma("sp", posi.ap, pos[b], LD("pos"), w=[posi])
                p.copy("dve", posf.ap, posi.ap, r=[posi], w=[posf])
                p.tt("dve", ang.ap, posf.ap.unsqueeze(2).broadcast_to([128, 32, 16]),
                     invf.ap.unsqueeze(1).broadcast_to([128, 32, 16]), ALU.mult, r=[posf, invf], w=[ang])
                for which in range(2):
                    if which == 0:
                        p.ts("dve", rr.ap, ang.ap, PI / 2, None, ALU.add, None, r=[ang], w=[rr])
                        src = rr
                    else:
                        src = ang
                    p.ts("dve", ki.ap, src.ap, float(1.0 / (2 * np.pi)), None, ALU.mult, None, r=[src], w=[ki])
                    p.copy("dve", kf.ap, ki.ap, r=[ki], w=[kf])
                    p.stt("dve", rr.ap, kf.ap, -6.28125, src.ap, ALU.mult, ALU.add, r=[kf, src], w=[rr])
                    p.stt("dve", rr.ap, kf.ap, -0.0019353071795864769, rr.ap, ALU.mult, ALU.add, r=[kf, rr], w=[rr])
                    p.ts("dve", kf.ap, rr.ap, PI, 2 * PI, ALU.is_gt, ALU.mult, r=[rr], w=[kf])
                    p.tt("dve", rr.ap, rr.ap, kf.ap, ALU.subtract, r=[rr, kf], w=[rr])
                    p.ts("dve", kf.ap, rr.ap, -PI, 2 * PI, ALU.is_lt, ALU.mult, r=[rr], w=[kf])
                    p.tt("dve", rr.ap, rr.ap, kf.ap, ALU.add, r=[rr, kf], w=[rr])
                    p.ts("dve", rr.ap, rr.ap, 3.1415925, -3.1415925, ALU.min, ALU.max, r=[rr], w=[rr])
                    c0 = 0 if which == 0 else 32
                    p.act(kf.ap, rr.ap, AF.Sin, r=[rr], w=[kf])
                    p.copy("dve", cs[:, :, c0:c0 + 16], kf.ap, r=[kf], w=[cs])
                    p.copy("dve", cs[:, :, c0 + 16:c0 + 32], kf.ap, r=[kf], w=[cs])

                ctx = {}
                if os.environ.get("KPRINT"):
                    print("A0 rope done, nops", p.nops)

                def s1(t, b=b):
                    s = t % 2
                    if os.environ.get("KPRINT"):
                        print("A0 s1", t, p.nops)
                    p.dma("sp", xt[s].ap, x[b, t * 128:(t + 1) * 128, :], LD("x%d" % s), w=[xt[s]])
                    jk = junks.next()
                    p.act(jk.ap, xt[s].ap, AF.Square, r=[xt[s]], w=[st1[s], jk], scale=1.0 / 32, accum=st1[s][:, 0:1])
                    rstd_chain(st1[s], [])
                    p.stt("dve", hb[s].ap, xt[s].ap, st1[s][:, 2:3], g0b.ap, ALU.mult, ALU.mult,
                          r=[xt[s], st1[s], g0b], w=[hb[s]])

                def s2(t, b=b):
                    s = t % 2
                    if os.environ.get("KPRINT"):
                        print("A0 s2", t, p.nops)
                    tb = T.next()
                    tv = tb.ap.bitcast(BF16)
                    p.tr([(tv[:, k * 128:(k + 1) * 128], hb[s][:, k * 128:(k + 1) * 128]) for k in range(8)],
                         idb.ap, r=[hb[s], idb], w=[tb])
                    p.copy("act", hT[s].ap.rearrange("p a b -> p (a b)"), tv, r=[tb], w=[hT[s]])
                    pj = [Fp.next() for i in range(4)]
                    bounds = [0, 416, 928, 1440, 1952]
                    for n in range(4):
                        c0, c1 = bounds[n], bounds[n + 1]
                        p.mm([(pj[n][:, 0:c1 - c0], hT[s][:, k, :], wi[:, k, c0:c1], k == 0, k == 7) for k in range(8)],
                             r=[hT[s], wi], w=[pj[n]])
                    jk = junks.next()
                    p.act(jk[:, 0:384], pj[0][:, 0:384], AF.Square, r=[pj[0]], w=[st2[s], jk],
                          scale=float(1.0 / np.sqrt(384.0)), accum=st2[s][:, 0:1])
                    jk = junks.next()
                    p.act(jk[:, 0:256], pj[1][:, 0:256], AF.Square, r=[pj[1]], w=[st2[s], jk],
                          scale=1.0 / 16, accum=st2[s][:, 1:2])
                    p.ts("dve", st2[s][:, 2:4], st2[s][:, 0:2], EPS, None, ALU.add, None, r=[st2[s]], w=[st2[s]])
                    p.tt("pool", st2[s][:, 4:6], st2[s][:, 2:4], nhalf[:, 0:2], ALU.pow, r=[st2[s], nhalf], w=[st2[s]])
                    p.stt("dve", cn[s][:, 0:384], pj[0][:, 0:384], st2[s][:, 4:5], gqb.ap, ALU.mult, ALU.mult,
                          r=[pj[0], st2[s], gqb], w=[cn[s]])
                    p.stt("dve", cn[s][:, 384:640], pj[1][:, 0:256], st2[s][:, 5:6], gkvb.ap, ALU.mult, ALU.mult,
                          r=[pj[1], st2[s], gkvb], w=[cn[s]])
                    kr = pj[0][:, 384:416]
                    p.tt("dve", tmpk[s][:, 0, :], kr, cs[:, t, 0:32], ALU.mult, r=[pj[0], cs], w=[tmpk[s]])
                    p.tt("dve", tmpk[s][:, 1, :], kr, cs[:, t, 32:64], ALU.mult, r=[pj[0], cs], w=[tmpk[s]])
                    p.tt("dve", ext[s][:, 0:16], tmpk[s][:, 0, 0:16], tmpk[s][:, 1, 16:32], ALU.subtract,
                         r=[tmpk[s]], w=[ext[s]])
                    p.tt("dve", ext[s][:, 16:32], tmpk[s][:, 0, 16:32], tmpk[s][:, 1, 0:16], ALU.add,
                         r=[tmpk[s]], w=[ext[s]])
                    p.copy("act", ext[s][:, 128:384], pj[1][:, 256:512], r=[pj[1]], w=[ext[s]])
                    p.act(sg[s][:, 0:512], pj[2].ap, AF.Silu, r=[pj[2]], w=[sg[s]])
                    p.act(sg[s][:, 512:1024], pj[3].ap, AF.Silu, r=[pj[3]], w=[sg[s]])
                    p.dma("sp", SG[0][b, t * 128:(t + 1) * 128, :], sg[s].ap, ST("sg%d" % s), r=[sg[s]])

                def s3(t, b=b):
                    s = t % 2
                    if os.environ.get("KPRINT"):
                        print("A0 s3", t, p.nops)
                    tb = T.next()
                    tv = tb.ap.bitcast(BF16)
                    items = [(tv[:, k * 128:(k + 1) * 128], cn[s][:, k * 128:(k + 1) * 128]) for k in range(5)]
                    items += [(tv[:, (5 + k) * 128:(6 + k) * 128], ext[s][:, k * 128:(k + 1) * 128]) for k in range(3)]
                    p.tr(items, idb.ap, r=[cn[s], ext[s], idb], w=[tb])
                    p.copy("dve", tT[s].ap.rearrange("p a b -> p (a b)"), tv, r=[tb], w=[tT[s]])
                    p.dma("sp", KR[b, :, t * 128:(t + 1) * 128], tT[s][0:32, 5, :], ST("tT%d" % s), r=[tT[s]])
                    for j in range(2):
                        p.dma("sp", QM[0][b, j, :, t * 128:(t + 1) * 128], tT[s][:, 6 + j, :], ST("tT%d" % s), r=[tT[s]])
                    for c in range(3):
                        fb = Fp.next()
                        p.mm([(fb[:, 0:384], tT[s][:, k, :], wq[:, k, c * 384:(c + 1) * 384], k == 0, k == 2)
                              for k in range(3)], r=[tT[s], wq], w=[fb])
                        p.copy("act", qb[s][:, c * 4:(c + 1) * 4, :].rearrange("p h d -> p (h d)"), fb[:, 0:384],
                               r=[fb], w=[qb[s]])
                        for hh in range(4):
                            h = c * 4 + hh
                            qr = fb[:, hh * 96 + 64:hh * 96 + 96]
                            p.tt("dve", rtmp[s][:, 0, hh, :], qr, cs[:, t, 0:32], ALU.mult, r=[fb, cs], w=[rtmp[s]])
                            p.tt("dve", rtmp[s][:, 1, hh, :], qr, cs[:, t, 32:64], ALU.mult, r=[fb, cs], w=[rtmp[s]])
                            p.tt("dve", qb[s][:, h, 64:80], rtmp[s][:, 0, hh, 0:16], rtmp[s][:, 1, hh, 16:32],
                                 ALU.subtract, r=[rtmp[s]], w=[qb[s]])
                            p.tt("dve", qb[s][:, h, 80:96], rtmp[s][:, 0, hh, 16:32], rtmp[s][:, 1, hh, 0:16],
                                 ALU.add, r=[rtmp[s]], w=[qb[s]])
                    for c in range(3):
                        fb = Fp.next()
                        p.mm([(fb.ap, tT[s][:, 3 + k, :], wkv[:, k, c * 512:(c + 1) * 512], k == 0, k == 1)
                              for k in range(2)], r=[tT[s], wkv], w=[fb])
                        p.copy("act", knb[s][:, c * 256:(c + 1) * 256], fb[:, 0:256], r=[fb], w=[knb[s]])
                        p.copy("dve", vb[s][:, c * 4:(c + 1) * 4, 0:64], fb[:, 256:512].rearrange("p (h d) -> p h d", h=4),
                               r=[fb], w=[vb[s]])
                    p.dma("sp", VV[b, t * 128:(t + 1) * 128, :], vb[s].ap.rearrange("p h d -> p (h d)"),
                          ST("vb%d" % s), r=[vb[s]])

                def s4(t, b=b):
                    s = t % 2
                    if os.environ.get("KPRINT"):
                        print("A0 s4", t, p.nops)
                    for (h0, h1) in ((0, 8), (8, 12)):
                        tb = T.next()
                        tv = tb.ap.bitcast(BF16)
                        p.tr([(tv[0:96, (h - h0) * 128:(h - h0 + 1) * 128], qb[s][:, h, :]) for h in range(h0, h1)],
                             idb.ap, r=[qb[s], idb], w=[tb])
                        n = h1 - h0
                        eng = "dve" if h0 == 0 else "act"
                        p.copy(eng, qT[s][0:96, h0:h1, :].rearrange("p h m -> p (h m)"), tv[0:96, 0:n * 128],
                               r=[tb], w=[qT[s]])
                    for h in range(12):
                        p.dma("sp", QT[b, h, :, t * 128:(t + 1) * 128], qT[s][0:96, h, :], ST("qT%d" % s), r=[qT[s]])
                    tb = T.next()
                    tv = tb.ap.bitcast(BF16)
                    p.tr([(tv[:, j * 128:(j + 1) * 128], knb[s][:, j * 128:(j + 1) * 128]) for j in range(6)],
                         idb.ap, r=[knb[s], idb], w=[tb])
                    p.copy("act", knT[s].ap.rearrange("p a b -> p (a b)"), tv[:, 0:768], r=[tb], w=[knT[s]])
                    for j in range(6):
                        p.dma("sp", KN[b, j, :, t * 128:(t + 1) * 128], knT[s][:, j, :], ST("knT%d" % s), r=[knT[s]])

                pipeline([s1, s2, s3, s4], NT)
            p.barrier()

        def phase_attn(layer, main):
            p.off = base_off
            Sp = Pool(p.banks[0:4])
            Ap = Pool(p.banks[4:6])
            Rp = Pool(p.banks[6:8])
            Vall = p.alloc([32, 12 * VW], BF16) if main else None
            KT = [p.alloc([4096], BF16) for i in range(2)]
            QTt = [p.alloc([4096], BF16) for i in range(2)]
            PT = [p.alloc([512], BF16) for i in range(4)]
            OT = [p.alloc([512], F32) for i in range(2)]
            ao = [p.alloc([4, 64], BF16) for i in range(2)]
            rd = [p.alloc([4], F32) for i in range(2)]
            hi = 0
            cnt = {"pt": 0, "o": 0}
            for b in range(NB):
                if main:
                    for c in range(4):
                        p.dma("sp", Vall[:, c * 8:(c + 1) * 8, :],
                              VV[b, c * 1024:(c + 1) * 1024, :].rearrange("(k p) f -> p k f", p=128),
                              LD("vall"), w=[Vall])
                heads = ([("main", h) for h in range(KHEADS)] if main else []) + [("mem", h) for h in range(4)]
                for (kind, h) in heads:
                    s = hi % 2
                    hi += 1
                    if kind == "main":
                        dd, nkt, scale, col0 = 96, 32, float(1.0 / np.sqrt(96.0)), h * 64
                        p.dma("sp", KT[s][0:64, :], KN[b, h // 2, (h % 2) * 64:(h % 2) * 64 + 64, :], LD("kt%d" % s), w=[KT[s]])
                        p.dma("sp", KT[s][64:96, :], KR[b, :, :], LD("kt%d" % s), w=[KT[s]])
                        p.dma("sp", QTt[s][0:96, :], QT[b, h, :, :], LD("qt%d" % s), w=[QTt[s]])
                    else:
                        dd, nkt, scale, col0 = 64, 2, 0.125, 768 + h * 64
                        p.dma("sp", QTt[s][0:64, :], QM[layer][b, h // 2, (h % 2) * 64:(h % 2) * 64 + 64, :],
                              LD("qt%d" % s), w=[QTt[s]])
                    for qc in range(KQC):
                        acc = Ap.next()
                        sb = {}
                        for kt in range(nkt + 2):
                            if kt < nkt:
                                sbk = Sp.next()
                                sb[kt] = sbk
                                if kind == "main":
                                    lhs, rk = KT[s][0:dd, kt * 128:(kt + 1) * 128], [KT[s]]
                                else:
                                    lhs, rk = Km[layer][b][0:64, h, kt * 128:(kt + 1) * 128], [Km[layer][b]]
                                p.mm([(sbk.ap, lhs, QTt[s][0:dd, qc * 512:(qc + 1) * 512], True, True)],
                                     r=rk + [QTt[s]], w=[sbk])
                            j = kt - 2
                            if j >= 0:
                                pt = PT[cnt["pt"] % 4]
                                cnt["pt"] += 1
                                p.act(pt.ap, sb[j].ap, AF.Exp, r=[sb[j]], w=[pt], scale=scale)
                                if kind == "main":
                                    lv, rv = Vall[:, j, h * VW:h * VW + 65], [Vall]
                                else:
                                    lv, rv = Vm[layer][b][:, j, h, 0:65], [Vm[layer][b]]
                                p.mm([(acc[0:65, :], lv, pt.ap, j == 0, j == nkt - 1)], r=rv + [pt], w=[acc])
                        o = cnt["o"] % 2
                        cnt["o"] += 1
                        p.copy("dve", OT[o][0:65, :], acc[0:65, :], r=[acc], w=[OT[o]])
                        rb = Rp.next()
                        rv_ = rb.ap[:, 0:320].rearrange("p (j d) -> p j d", j=4)
                        p.tr([(rv_[:, j, 0:65], OT[o][0:65, j * 128:(j + 1) * 128]) for j in range(4)],
                             idf.ap, r=[OT[o], idf], w=[rb])
                        for j in range(4):
                            p.op("dve", lambda e, o=o, rv_=rv_, j=j: e.reciprocal(out=rd[o][:, j:j + 1], in_=rv_[:, j, 64:65]),
                                 r=[rb], w=[rd[o]])
                            p.ts("dve", ao[o][:, j, :], rv_[:, j, 0:64], rd[o][:, j:j + 1], None, ALU.mult, None,
                                 r=[rb, rd[o]], w=[ao[o]])
                            p.dma("sp", ATT[layer][b, qc * 512 + j * 128:qc * 512 + (j + 1) * 128, col0:col0 + 64],
                                  ao[o][:, j, :], ST("ao%d" % o), r=[ao[o]])
            p.barrier()

        def phase_C0A1():
            p.off = base_off
            T = Pool(p.banks[0:2])
            Fp = Pool(p.banks[2:8])
            wo = p.alloc([8, 1024], BF16)
            wi1 = p.alloc([8, 2048], BF16)
            p.group_begin()
            load_w(wo, w_out0, 8, "w")
            load_w(wi1, w_in1, 8, "w")
            g1b = p.alloc([1024], F32)
            p.dma("sp", g1b.ap, g1[:, :], LD("g"), w=[g1b])
            ccs = [p.alloc([2, 192], F32), p.alloc([2, 192], F32)]
            wf = p.alloc([2, 768], F32)
            m12 = p.alloc([4, 2, 384], BF16)
            for i, src in enumerate((c_cc, c_sc)):
                p.dma("sp", ccs[i][:, 0, :], src[0:128, :], LD("g"), w=[ccs[i]])
                p.dma("sp", ccs[i][0:64, 1, :], src[128:192, :], LD("g"), w=[ccs[i]])
            p.dma("sp", wf[:, 0, :], w_fn[0:128, :], LD("g"), w=[wf])
            p.dma("sp", wf[0:64, 1, :], w_fn[128:192, :], LD("g"), w=[wf])
            p.group_end()
            for g in range(4):
                for i in range(2):
                    for mc, (m0, m1) in enumerate(((0, 128), (128, 192))):
                        fb = Fp.next()
                        mrows = m1 - m0
                        p.mm([(fb[0:mrows, 0:192], ccs[i][0:128, 0, m0:m1], wf[0:128, 0, g * 192:(g + 1) * 192], True, False),
                              (fb[0:mrows, 0:192], ccs[i][0:64, 1, m0:m1], wf[0:64, 1, g * 192:(g + 1) * 192], False, True)],
                             r=[ccs[i], wf], w=[fb])
                        if i == 0:
                            p.copy("dve", m12[0:mrows, g, mc, 0:192], fb[0:mrows, 0:192], r=[fb], w=[m12])
                        else:
                            p.ts("dve", m12[0:mrows, g, mc, 192:384], fb[0:mrows, 0:192], -1.0, None, ALU.mult, None,
                                 r=[fb], w=[m12])
            at = [p.alloc([1024], BF16) for i in range(2)]
            sgt = [p.alloc([1024], BF16) for i in range(2)]
            xt = [p.alloc([1024], F32) for i in range(2)]
            br = [p.alloc([1024], BF16) for i in range(2)]
            brT = [p.alloc([8, 128], BF16) for i in range(2)]
            x1t = [p.alloc([1024], F32) for i in range(2)]
            st1 = [p.alloc([8], F32) for i in range(2)]
            hb = [p.alloc([1024], BF16) for i in range(2)]
            hT = [p.alloc([8, 128], BF16) for i in range(2)]
            fbq = [p.alloc([1024], BF16) for i in range(2)]
            sg1 = [p.alloc([1024], BF16) for i in range(2)]
            fT = [p.alloc([8, 128], BF16) for i in range(2)]
            qmT = [p.alloc([2, 128], BF16) for i in range(2)]
            zb = [p.alloc([2, 4, 192], BF16) for i in range(2)]

            for b in range(NB):
                def s1(t, b=b):
                    s = t % 2
                    rows = slice(t * 128, (t + 1) * 128)
                    p.dma("sp", at[s].ap, ATT[0][b, rows, :], LD("at%d" % s), w=[at[s]])
                    p.dma("sp", sgt[s].ap, SG[0][b, rows, :], LD("sgt%d" % s), w=[sgt[s]])
                    p.dma("sp", xt[s].ap, x[b, rows, :], LD("x%d" % s), w=[xt[s]])
                    p.tt("dve", br[s].ap, at[s].ap, sgt[s].ap, ALU.mult, r=[at[s], sgt[s]], w=[br[s]])

                def s2(t, b=b):
                    s = t % 2
                    rows = slice(t * 128, (t + 1) * 128)
                    tb = T.next()
                    tv = tb.ap.bitcast(BF16)
                    p.tr([(tv[:, k * 128:(k + 1) * 128], br[s][:, k * 128:(k + 1) * 128]) for k in range(8)],
                         idb.ap, r=[br[s], idb], w=[tb])
                    p.copy("act", brT[s].ap.rearrange("p a b -> p (a b)"), tv, r=[tb], w=[brT[s]])
                    for n in range(2):
                        fb = Fp.next()
                        p.mm([(fb.ap, brT[s][:, k, :], wo[:, k, n * 512:(n + 1) * 512], k == 0, k == 7) for k in range(8)],
                             r=[brT[s], wo], w=[fb])
                        p.tt("dve", x1t[s][:, n * 512:(n + 1) * 512], fb.ap, xt[s][:, n * 512:(n + 1) * 512], ALU.add,
                             r=[fb, xt[s]], w=[x1t[s]])
                    p.dma("sp", X1[b, rows, :], x1t[s].ap, ST("x1%d" % s), r=[x1t[s]])
                    jk = junks.next()
                    p.act(jk.ap, x1t[s].ap, AF.Square, r=[x1t[s]], w=[st1[s], jk], scale=1.0 / 32, accum=st1[s][:, 0:1])
                    rstd_chain(st1[s], [])
                    p.stt("dve", hb[s].ap, x1t[s].ap, st1[s][:, 2:3], g1b.ap, ALU.mult, ALU.mult,
                          r=[x1t[s], st1[s], g1b], w=[hb[s]])

                def s3(t, b=b):
                    s = t % 2
                    rows = slice(t * 128, (t + 1) * 128)
                    tb = T.next()
                    tv = tb.ap.bitcast(BF16)
                    p.tr([(tv[:, k * 128:(k + 1) * 128], hb[s][:, k * 128:(k + 1) * 128]) for k in range(8)],
                         idb.ap, r=[hb[s], idb], w=[tb])
                    p.copy("act", hT[s].ap.rearrange("p a b -> p (a b)"), tv, r=[tb], w=[hT[s]])
                    pj = [Fp.next() for i in range(4)]
                    for n in range(4):
                        p.mm([(pj[n].ap, hT[s][:, k, :], wi1[:, k, n * 512:(n + 1) * 512], k == 0, k == 7) for k in range(8)],
                             r=[hT[s], wi1], w=[pj[n]])
                    p.copy("dve", fbq[s][:, 0:512], pj[0].ap, r=[pj[0]], w=[fbq[s]])
                    p.copy("dve", fbq[s][:, 512:1024], pj[1].ap, r=[pj[1]], w=[fbq[s]])
                    p.act(sg1[s][:, 0:512], pj[2].ap, AF.Silu, r=[pj[2]], w=[sg1[s]])
                    p.act(sg1[s][:, 512:1024], pj[3].ap, AF.Silu, r=[pj[3]], w=[sg1[s]])
                    p.dma("sp", SG[1][b, rows, :], sg1[s].ap, ST("sg1%d" % s), r=[sg1[s]])

                def s4(t, b=b):
                    s = t % 2
                    rows = slice(t * 128, (t + 1) * 128)
                    tb = T.next()
                    tv = tb.ap.bitcast(BF16)
                    items = []
                    for g in range(4):
                        items.append((tv[:, (2 * g) * 128:(2 * g + 1) * 128], fbq[s][:, g * 192:g * 192 + 128]))
                        items.append((tv[0:64, (2 * g + 1) * 128:(2 * g + 2) * 128], fbq[s][:, g * 192 + 128:g * 192 + 192]))
                    p.tr(items, idb.ap, r=[fbq[s], idb], w=[tb])
                    p.copy("act", fT[s].ap.rearrange("p a b -> p (a b)"), tv, r=[tb], w=[fT[s]])
                    tb2 = T.next()
                    tv2 = tb2.ap.bitcast(BF16)
                    p.tr([(tv2[:, j * 128:(j + 1) * 128], fbq[s][:, 768 + j * 128:768 + (j + 1) * 128]) for j in range(2)],
                         idb.ap, r=[fbq[s], idb], w=[tb2])
                    p.copy("dve", qmT[s].ap.rearrange("p a b -> p (a b)"), tv2[:, 0:256], r=[tb2], w=[qmT[s]])
                    for j in range(2):
                        p.dma("sp", QM[1][b, j, :, rows], qmT[s][:, j, :], ST("qmT%d" % s), r=[qmT[s]])
                    for g in range(4):
                        fb = Fp.next()
                        p.mm([(fb[:, 0:384], fT[s][:, 2 * g, :], m12[:, g, 0, :], True, False),
                              (fb[:, 0:384], fT[s][0:64, 2 * g + 1, :], m12[0:64, g, 1, :], False, True)],
                             r=[fT[s], m12], w=[fb])
                        if g % 2:
                            for ri in range(2):
                                p.copy("act", zb[s][:, ri, g, :], fb[:, ri * 192:(ri + 1) * 192], r=[fb], w=[zb[s]])
                        else:
                            p.copy("dve", zb[s][:, :, g, :], fb[:, 0:384].rearrange("p (r c) -> p r c", r=2),
                                   r=[fb], w=[zb[s]])
                    p.dma("sp", ZZ[b, rows, :, :].rearrange("p r c -> p (r c)"), zb[s].ap.rearrange("p r g c -> p (r g c)"),
                          ST("zb%d" % s), r=[zb[s]])

                pipeline([s1, s2, s3, s4], NT)
            p.barrier()

        def phase_F():
            p.off = base_off
            Fp = Pool(p.banks[0:8])
            t1 = p.alloc([64, 128], BF16)
            d2 = p.alloc([64], BF16)
            p.group_begin()
            for c in range(4):
                p.dma("sp", t1[:, c * 16:(c + 1) * 16, :], c_t1[:, c * 16:(c + 1) * 16, :], LD("c"), w=[t1])
            p.dma("sp", d2.ap, c_d2[:, :], LD("c"), w=[d2])
            p.group_end()
            NS = 4
            zin = [p.alloc([NS, 768], BF16) for i in range(2)]
            bsb = [p.alloc([NS, 768], BF16) for i in range(2)]
            bt = [p.alloc([NS, 768], BF16) for i in range(2)]
            xsb = [p.alloc([NS, 768], BF16) for i in range(2)]
            for b in range(NB):
                zsrc = ZZ[b].rearrange("(s1 s2) r c -> r s1 s2 c", s2=64)
                for ci in range(64 // NS):
                    s = ci % 2
                    for ri in range(2):
                        p.dma("sp", zin[s][ri * 64:(ri + 1) * 64, :, :], zsrc[ri, :, ci * NS:(ci + 1) * NS, :],
                              LD("zin%d" % s), w=[zin[s]])
                    for j in range(NS):
                        s2_ = ci * NS + j
                        for hf in range(2):
                            fb = Fp.next()
                            p.mm([(fb[:, 0:384], t1[:, s2_, :], zin[s][:, j, hf * 384:(hf + 1) * 384], True, True)],
                                 r=[t1, zin[s]], w=[fb])
                            p.copy("act" if hf else "dve", bsb[s][:, j, hf * 384:(hf + 1) * 384], fb[:, 0:384],
                                   r=[fb], w=[bsb[s]])
                    for ri in range(2):
                        p.dma("sp", BS[b, ri, ci * NS:(ci + 1) * NS, :, :].rearrange("s k c -> k s c"),
                              bsb[s][ri * 64:(ri + 1) * 64, :, :], ST("bsb%d" % s), r=[bsb[s]])
                p.barrier()
                for ci in range(64 // NS):
                    s = ci % 2
                    for ri in range(2):
                        p.dma("sp", bt[s][ri * 64:(ri + 1) * 64, :, :], BS[b, ri, :, ci * NS:(ci + 1) * NS, :],
                              LD("bt%d" % s), w=[bt[s]])
                    for j in range(NS):
                        for hf in range(2):
                            fb = Fp.next()
                            p.mm([(fb[0:64, 0:384], d2.ap, bt[s][:, j, hf * 384:(hf + 1) * 384], True, True)],
                                 r=[d2, bt[s]], w=[fb])
                            p.copy("act" if hf else "dve", xsb[s][0:64, j, hf * 384:(hf + 1) * 384], fb[0:64, 0:384],
                                   r=[fb], w=[xsb[s]])
                    dst = ATT[1][b].rearrange("(k2 k1) c -> k2 k1 c", k1=64)[:, ci * NS:(ci + 1) * NS, 0:768]
                    p.dma("sp", dst, xsb[s][0:64, :, :], ST("xsb%d" % s), r=[xsb[s]])
            p.barrier()

        def phase_C1():
            p.off = base_off
            T = Pool(p.banks[0:2])
            Fp = Pool(p.banks[2:8])
            wo = p.alloc([8, 1024], BF16)
            p.group_begin()
            load_w(wo, w_out1, 8, "w")
            gfb = p.alloc([1024], F32)
            p.dma("sp", gfb.ap, gf[:, :], LD("g"), w=[gfb])
            p.group_end()
            at = [p.alloc([1024], BF16) for i in range(2)]
            sgt = [p.alloc([1024], BF16) for i in range(2)]
            xt = [p.alloc([1024], F32) for i in range(2)]
            br = [p.alloc([1024], BF16) for i in range(2)]
            brT = [p.alloc([8, 128], BF16) for i in range(2)]
            x2t = [p.alloc([1024], F32) for i in range(2)]
            st1 = [p.alloc([8], F32) for i in range(2)]
            yt = [p.alloc([1024], F32) for i in range(2)]
            for b in range(NB):
                def s1(t, b=b):
                    s = t % 2
                    rows = slice(t * 128, (t + 1) * 128)
                    p.dma("sp", at[s].ap, ATT[1][b, rows, :], LD("at%d" % s), w=[at[s]])
                    p.dma("sp", sgt[s].ap, SG[1][b, rows, :], LD("sgt%d" % s), w=[sgt[s]])
                    p.dma("sp", xt[s].ap, X1[b, rows, :], LD("x%d" % s), w=[xt[s]])
                    p.tt("dve", br[s].ap, at[s].ap, sgt[s].ap, ALU.mult, r=[at[s], sgt[s]], w=[br[s]])

                def s2(t, b=b):
                    s = t % 2
                    rows = slice(t * 128, (t + 1) * 128)
                    tb = T.next()
                    tv = tb.ap.bitcast(BF16)
                    p.tr([(tv[:, k * 128:(k + 1) * 128], br[s][:, k * 128:(k + 1) * 128]) for k in range(8)],
                         idb.ap, r=[br[s], idb], w=[tb])
                    p.copy("act", brT[s].ap.rearrange("p a b -> p (a b)"), tv, r=[tb], w=[brT[s]])
                    for n in range(2):
                        fb = Fp.next()
                        p.mm([(fb.ap, brT[s][:, k, :], wo[:, k, n * 512:(n + 1) * 512], k == 0, k == 7) for k in range(8)],
                             r=[brT[s], wo], w=[fb])
                        p.tt("dve", x2t[s][:, n * 512:(n + 1) * 512], fb.ap, xt[s][:, n * 512:(n + 1) * 512], ALU.add,
                             r=[fb, xt[s]], w=[x2t[s]])
                    jk = junks.next()
                    p.act(jk.ap, x2t[s].ap, AF.Square, r=[x2t[s]], w=[st1[s], jk], scale=1.0 / 32, accum=st1[s][:, 0:1])
                    rstd_chain(st1[s], [])
                    p.stt("dve", yt[s].ap, x2t[s].ap, st1[s][:, 2:3], gfb.ap, ALU.mult, ALU.mult,
                          r=[x2t[s], st1[s], gfb], w=[yt[s]])
                    p.dma("sp", y[b, rows, :], yt[s].ap, ST("y%d" % s), r=[yt[s]])

                pipeline([s1, s2], NT)
            p.barrier()

        phases = [("M", phase_M), ("A0", phase_A0), ("B0", lambda: phase_attn(0, True)), ("C0A1", phase_C0A1),
                  ("B1", lambda: phase_attn(1, False)), ("F", phase_F), ("C1", phase_C1)]
        for name, fn in phases:
            if stop_after == "S":
                break
            fn()
            if stop_after == name:
                break
        p.barrier()
        p.emit()
    return nc


def _kc_layout(w, nk):
    k, n = w.shape
    return np.ascontiguousarray(w.reshape(nk, 128, n).transpose(1, 0, 2))


def _bcast(g):
    return np.ascontiguousarray(np.broadcast_to(np.asarray(g, np.float32)[None, :], (128, g.shape[0])))


def _constants():
    c = {}
    c["c_idb"] = np.eye(128, dtype=np.float32).astype(BF)
    c["c_idf"] = np.eye(128, dtype=np.float32)
    invf = (1.0 / (10000.0 ** (np.arange(0, 32, 2, dtype=np.float32) / 32))).astype(np.float32)
    c["c_invf"] = _bcast(invf)
    cc = np.arange(192)
    th = 2 * np.pi * np.outer(cc, cc) / 192.0
    sc = 1.0 / np.sqrt(4096.0 * 192.0)
    c["c_cc"] = (np.cos(th) * sc).astype(np.float32)
    c["c_sc"] = (np.sin(th) * sc).astype(np.float32)
    s1 = np.arange(64)[:, None, None]
    s2 = np.arange(64)[None, :, None]
    k1 = np.arange(64)[None, None, :]
    ph = 2 * np.pi * (s1 * k1 / 64.0 + s2 * k1 / 4096.0)
    gr, gi = np.cos(ph), -np.sin(ph)
    t1 = np.zeros((128, 64, 128), np.float64)
    t1[0:64, :, 0:64] = gr
    t1[64:128, :, 0:64] = -gi
    t1[0:64, :, 64:128] = gi
    t1[64:128, :, 64:128] = gr
    c["c_t1"] = t1.astype(np.float32).astype(BF)
    ph2 = 2 * np.pi * np.outer(np.arange(64), np.arange(64)) / 64.0
    d2 = np.concatenate([np.cos(ph2), np.sin(ph2)], axis=0)
    c["c_d2"] = d2.astype(np.float32).astype(BF)
    return c


_NC_CACHE = {}
_UKV_PERM = np.concatenate([np.concatenate([np.arange(h * 128, h * 128 + 64) for h in range(4 * c, 4 * c + 4)] +
                                           [np.arange(h * 128 + 64, h * 128 + 128) for h in range(4 * c, 4 * c + 4)])
                            for c in range(3)])


def _prep_inputs(inputs):
    f = lambda a: np.ascontiguousarray(np.asarray(a, dtype=np.float32))
    w_in0 = f(inputs["w_in_l0"])
    perm = np.concatenate([np.arange(0, 384), np.arange(640, 672), np.arange(384, 640), np.arange(672, 928),
                           np.arange(928, 1952)])
    shared = {
        "w_in0": _kc_layout(w_in0[:, perm], 8),
        "w_uq": _kc_layout(f(inputs["w_uq_l0"]), 3),
        "w_ukv": _kc_layout(f(inputs["w_ukv_l0"])[:, _UKV_PERM], 2),
        "w_out0": _kc_layout(f(inputs["w_out_l0"]), 8),
        "w_out1": _kc_layout(f(inputs["w_out_l1"]), 8),
        "w_mem0": _kc_layout(f(inputs["w_mem_kv_l0"]), 8),
        "w_mem1": _kc_layout(f(inputs["w_mem_kv_l1"]), 8),
        "w_in1": _kc_layout(f(inputs["w_in_l1"]), 8),
        "w_fn": np.ascontiguousarray(f(inputs["w_fnet_l1"]).transpose(1, 0, 2).reshape(192, 768)),
        "g0": _bcast(f(inputs["norm_g_l0"])),
        "g1": _bcast(f(inputs["norm_g_l1"])),
        "gf": _bcast(f(inputs["final_norm_g"])),
        "gm0": _bcast(f(inputs["mem_norm_g_l0"])),
        "gm1": _bcast(f(inputs["mem_norm_g_l1"])),
        "gq": _bcast(f(inputs["q_norm_g_l0"])),
        "gkv": _bcast(f(inputs["kv_norm_g_l0"])),
    }
    shared.update(_constants())
    xs = f(inputs["x"])
    mems = f(inputs["mem"])
    poss = np.asarray(inputs["positions"]).astype(np.int32)
    in_maps = []
    for c in range(8):
        m = dict(shared)
        m["x"] = np.ascontiguousarray(xs[2 * c:2 * c + 2])
        m["mem"] = np.ascontiguousarray(mems[2 * c:2 * c + 2])
        m["pos"] = np.ascontiguousarray(poss[2 * c:2 * c + 2].reshape(2, 32, 128).transpose(0, 2, 1))
        in_maps.append(m)
    return in_maps


def kernel(**inputs):
    in_maps = _prep_inputs(inputs)
    if "nc" not in _NC_CACHE:
        _NC_CACHE["nc"] = build_program()
    res = run_bass_kernel_spmd(_NC_CACHE["nc"], in_maps, core_ids=list(range(8)))
    out = np.concatenate([np.asarray(r["y"]) for r in res.results], axis=0)
    return out.astype(np.float32)
```
